# Optimizing a Trainium2 kernel written in Bass

```python
import math
import jax, jax.numpy as jnp
from jax import lax
import numpy as np

D_MODEL = 1024
BATCH = 2
SEQ = 8192
DEPTH = 2

N_MIXERS = 2
HEAD_DIM = 64
V_DIM = 2 * HEAD_DIM
N_HEADS = D_MODEL // V_DIM
QK_WIDTH = N_HEADS * 2 * HEAD_DIM
V_WIDTH = N_HEADS * V_DIM
ROT_DIM = HEAD_DIM // 4
ROPE_THETA = 500000.0
Q_BLOCK = 128
CONV_WIDTH = 3
CONV_DIM = D_MODEL
D_FF = 4 * D_MODEL
EPS = 1e-6
N_ATTN = (DEPTH + 1) // 2
N_CONV = DEPTH // 2

kernel_name = "hybrid_diffattn_shortconv_sandwich"


def rmsnorm(x, g):
    xf = x.astype(jnp.float32)
    y = xf * lax.rsqrt(jnp.mean(xf * xf, axis=-1, keepdims=True) + EPS)
    return (y * g.astype(jnp.float32)).astype(x.dtype)


def rope_tables(seq_len):
    pos = jnp.arange(seq_len, dtype=jnp.float32)
    inv_freq = ROPE_THETA ** (-jnp.arange(0, ROT_DIM, 2, dtype=jnp.float32) / ROT_DIM)
    ang = pos[:, None] * inv_freq[None, :]
    return jnp.cos(ang), jnp.sin(ang)


def apply_partial_rope(t, cos, sin):
    half = ROT_DIM // 2
    x1 = t[..., :half].astype(jnp.float32)
    x2 = t[..., half:ROT_DIM].astype(jnp.float32)
    c = cos[None, :, None, None, :]
    s = sin[None, :, None, None, :]
    rot = jnp.concatenate([x1 * c - x2 * s, x2 * c + x1 * s], axis=-1).astype(t.dtype)
    return jnp.concatenate([rot, t[..., ROT_DIM:]], axis=-1)


def lambda_init_fn(layer_idx):
    return 0.8 - 0.6 * math.exp(-0.3 * layer_idx)


def diff_attention(x, w_qkv, w_o, lq1, lk1, lq2, lk2, subln_g, lambda_init):
    B, S, _ = x.shape
    qkv = x @ w_qkv
    q = qkv[..., :QK_WIDTH].reshape(B, S, N_HEADS, 2, HEAD_DIM)
    k = qkv[..., QK_WIDTH:2 * QK_WIDTH].reshape(B, S, N_HEADS, 2, HEAD_DIM)
    v = qkv[..., 2 * QK_WIDTH:].reshape(B, S, N_HEADS, V_DIM)
    cos, sin = rope_tables(S)
    q = apply_partial_rope(q, cos, sin) * (HEAD_DIM ** -0.5)
    k = apply_partial_rope(k, cos, sin)
    lam = (jnp.exp(jnp.sum(lq1.astype(jnp.float32) * lk1.astype(jnp.float32)))
           - jnp.exp(jnp.sum(lq2.astype(jnp.float32) * lk2.astype(jnp.float32)))
           + lambda_init)
    n_blocks = S // Q_BLOCK
    qb = q.reshape(B, n_blocks, Q_BLOCK, N_HEADS, 2, HEAD_DIM).transpose(1, 0, 2, 3, 4, 5)
    k32 = k.astype(jnp.float32)
    key_pos = jnp.arange(S)

    def block(args):
        q_i, i = args
        s = jnp.einsum('bqhcd,bkhcd->bhcqk', q_i.astype(jnp.float32), k32)
        q_pos = i * Q_BLOCK + jnp.arange(Q_BLOCK)
        mask = key_pos[None, :] <= q_pos[:, None]
        s = jnp.where(mask, s, -jnp.inf)
        p = jax.nn.softmax(s, axis=-1)
        a = p[:, :, 0] - lam * p[:, :, 1]
        return jnp.einsum('bhqk,bkhe->bqhe', a.astype(v.dtype), v)

    o = lax.map(block, (qb, jnp.arange(n_blocks)))
    o = o.transpose(1, 0, 2, 3, 4).reshape(B, S, N_HEADS, V_DIM)
    o = rmsnorm(o, subln_g) * (1.0 - lambda_init)
    return o.reshape(B, S, V_WIDTH) @ w_o


def short_conv(x, w_in, conv_w, w_out):
    h = x @ w_in
    b_gate = h[..., :CONV_DIM]
    c_gate = h[..., CONV_DIM:2 * CONV_DIM]
    u = c_gate * h[..., 2 * CONV_DIM:]
    up = jnp.pad(u, ((0, 0), (CONV_WIDTH - 1, 0), (0, 0)))
    S = x.shape[1]
    y = (conv_w[0] * up[:, 0:S] + conv_w[1] * up[:, 1:S + 1] + conv_w[2] * up[:, 2:S + 2])
    return (b_gate * y) @ w_out


def sq_relu_mlp(x, w_up, w_down):
    return jnp.square(jax.nn.relu(x @ w_up)) @ w_down


def setup_inputs(seed: int = 0) -> dict:
    key = jax.random.key(seed)
    ks = jax.random.split(key, 20)
    f32 = jnp.float32
    D = D_MODEL

    def nrm(k, shape, scale):
        return jax.random.normal(k, shape, f32) * scale

    return {
        "x": nrm(ks[0], (BATCH, SEQ, D), 1.0),
        "attn_w_qkv": nrm(ks[1], (N_ATTN, D, 2 * QK_WIDTH + V_WIDTH), D ** -0.5),
        "attn_w_o": nrm(ks[2], (N_ATTN, V_WIDTH, D), V_WIDTH ** -0.5),
        "attn_lambda_q1": nrm(ks[3], (N_ATTN, HEAD_DIM), 0.1),
        "attn_lambda_k1": nrm(ks[4], (N_ATTN, HEAD_DIM), 0.1),
        "attn_lambda_q2": nrm(ks[5], (N_ATTN, HEAD_DIM), 0.1),
        "attn_lambda_k2": nrm(ks[6], (N_ATTN, HEAD_DIM), 0.1),
        "attn_subln_g": 1.0 + nrm(ks[7], (N_ATTN, V_DIM), 0.02),
        "conv_w_in": nrm(ks[8], (N_CONV, D, 3 * CONV_DIM), D ** -0.5),
        "conv_w": nrm(ks[9], (N_CONV, CONV_WIDTH, CONV_DIM), CONV_WIDTH ** -0.5),
        "conv_w_out": nrm(ks[10], (N_CONV, CONV_DIM, D), CONV_DIM ** -0.5),
        "mlp_w_up": nrm(ks[11], (DEPTH, D, D_FF), D ** -0.5),
        "mlp_w_down": nrm(ks[12], (DEPTH, D_FF, D), D_FF ** -0.5),
        "norm_mixer_pre": 1.0 + nrm(ks[13], (DEPTH, D), 0.02),
        "norm_mixer_post": 1.0 + nrm(ks[14], (DEPTH, D), 0.02),
        "norm_mlp_pre": 1.0 + nrm(ks[15], (DEPTH, D), 0.02),
        "norm_mlp_post": 1.0 + nrm(ks[16], (DEPTH, D), 0.02),
    }


def reference(x, attn_w_qkv, attn_w_o, attn_lambda_q1, attn_lambda_k1, attn_lambda_q2,
              attn_lambda_k2, attn_subln_g, conv_w_in, conv_w, conv_w_out, mlp_w_up,
              mlp_w_down, norm_mixer_pre, norm_mixer_post, norm_mlp_pre, norm_mlp_post):
    for i in range(DEPTH):
        h = rmsnorm(x, norm_mixer_pre[i])
        j = i // N_MIXERS
        if i % N_MIXERS == 0:
            m = diff_attention(h, attn_w_qkv[j], attn_w_o[j], attn_lambda_q1[j],
                               attn_lambda_k1[j], attn_lambda_q2[j], attn_lambda_k2[j],
                               attn_subln_g[j], lambda_init_fn(i))
        else:
            m = short_conv(h, conv_w_in[j], conv_w[j], conv_w_out[j])
        x = x + rmsnorm(m, norm_mixer_post[i])
        h = rmsnorm(x, norm_mlp_pre[i])
        x = x + rmsnorm(sq_relu_mlp(h, mlp_w_up[i], mlp_w_down[i]), norm_mlp_post[i])
    return x
```

```python
import contextlib
import math

import numpy as np
import ml_dtypes

import concourse.bass as bass
import concourse.mybir as mybir
from concourse.bass_utils import run_bass_kernel_spmd

F32 = mybir.dt.float32
BF16 = mybir.dt.bfloat16
AF = mybir.ActivationFunctionType
ALU = mybir.AluOpType

D = 1024
NT = 16
TOK = NT * 128
NH = 8
EPS = 1e-6
DFF = 4096
LAMBDA_INIT0 = 0.8 - 0.6 * math.exp(-0.3 * 0)

ENGS = ("sp", "act", "pe", "dve", "pool")
import os
STAGE = int(os.environ.get('P1_STAGE', '9'))


class Ctx:
    def __init__(self, nc, stack):
        self.nc = nc
        self.stack = stack
        self.sem = {}
        self.cnt = {}
        self.waited = {e: {} for e in ENGS}
        self.ops = {e: [] for e in ENGS}
        for e in ENGS:
            self.newsem(e)

    def newsem(self, name):
        if name not in self.sem:
            self.sem[name] = self.stack.enter_context(self.nc.semaphore("s_" + name))
            self.cnt[name] = 0
        return name

    def op(self, eng, fn, waits=(), sig=None, inc=1):
        tok = None
        if sig is True:
            sig = eng
        if sig is not None:
            self.cnt[sig] += inc
            tok = (sig, self.cnt[sig])
        self.ops[eng].append((fn, tuple(w for w in waits if w is not None), sig, inc))
        return tok

    def wait(self, eng, waits):
        self.ops[eng].append((None, tuple(w for w in waits if w is not None), None, 0))

    def dma(self, eng, out, in_, sem, waits=()):
        self.newsem(sem)
        return self.op(eng, lambda e, o=out, i=in_: e.dma_start(out=o, in_=i), waits, sig=sem, inc=16)

    def last(self, name):
        return (name, self.cnt[name]) if self.cnt[name] > 0 else None

    def flush(self):
        ops, self.ops = self.ops, {e: [] for e in ENGS}
        with self.nc.Block() as block:
            decos = {"sp": block.sync, "act": block.scalar, "pe": block.tensor,
                     "dve": block.vector, "pool": block.gpsimd}
            for ename in ENGS:
                lst = ops[ename]
                if not lst:
                    continue

                def body(e, lst=lst, ename=ename):
                    wd = self.waited[ename]
                    for fn, waits, sig, inc in lst:
                        for (s, v) in waits:
                            if wd.get(s, 0) < v:
                                e.wait_ge(self.sem[s], v)
                                wd[s] = v
                        if fn is None:
                            continue
                        ins = fn(e)
                        if sig is not None:
                            ins.then_inc(self.sem[sig], inc)

                decos[ename](body)


class Rot:
    def __init__(self, bufs):
        self.bufs = bufs
        self.free = [[] for _ in bufs]
        self.i = 0

    def next(self):
        k = self.i % len(self.bufs)
        self.i += 1
        fr = self.free[k]
        self.free[k] = []
        return k, self.bufs[k], fr

    def release(self, k, *toks):
        self.free[k].extend(t for t in toks if t is not None)


def sb(nc, stack, name, shape, dt):
    return stack.enter_context(nc.sbuf_tensor("sb_" + name, list(shape), dt))


def ps(nc, stack, name, shape, dt):
    return stack.enter_context(nc.psum_tensor("ps_" + name, list(shape), dt))


def phase1(c, nc, io, QT, KT, Vs):
    st = contextlib.ExitStack()
    with st:
        wq = sb(nc, st, "wq", [128, 8, 3072], BF16)
        grow = sb(nc, st, "grow", [128, 1024], F32)
        cos_sb = sb(nc, st, "cos_sb", [128, NT, 8], F32)
        sin_sb = sb(nc, st, "sin_sb", [128, NT, 8], F32)
        ident = sb(nc, st, "ident", [128, 128], BF16)
        epsb = sb(nc, st, "epsb", [128, 1], F32)
        xt = [sb(nc, st, f"xt{i}", [128, 1024], F32) for i in range(2)]
        junk = sb(nc, st, "junk", [128, 1024], BF16)
        ss = sb(nc, st, "ss", [128, NT], F32)
        sse = sb(nc, st, "sse", [128, NT], F32)
        rstd = sb(nc, st, "rstd", [128, NT], F32)
        xn = [sb(nc, st, f"xn{i}", [128, 1024], BF16) for i in range(2)]
        hT = [sb(nc, st, f"hT{i}", [128, 8, 128], BF16) for i in range(2)]
        qk = [sb(nc, st, f"qk{i}", [128, 2, 1024], BF16) for i in range(2)]
        rt = [[sb(nc, st, f"rt{i}_{k}", [128, 16, 8], F32) for k in range(4)] for i in range(2)]
        tp = [ps(nc, st, f"tp{i}", [128, 1024], BF16) for i in range(2)]
        acc = [ps(nc, st, f"acc{i}", [128, 1024], F32) for i in range(3)]

        t_w = []
        wv = io["wqkv"].rearrange("(c p) n -> p c n", p=128)
        for cc in range(8):
            t_wl = c.dma("pool", wq[:, cc, :], wv[:, cc, :], "ld_w")
        t_w = [t_wl] * 8
        t_id = c.dma("pool", ident[:, :], io["ident"][:, :], "ld_c")
        t_g = c.dma("sp", grow[:, :], io["g_pre0_row"][0:1, :].to_broadcast([128, 1024]), "ld_c2")
        t_cs = c.dma("sp", cos_sb[:, :, :], io["cos_t"].rearrange("p (m k) -> p m k", k=8), "ld_c2")
        t_sn = c.dma("sp", sin_sb[:, :, :], io["sin_t"].rearrange("p (m k) -> p m k", k=8), "ld_c2")
        t_g = t_cs = t_sn
        t_z = c.op("dve", lambda e: e.memset(ss[:, :], 0.0), sig=True)
        t_nh = c.op("dve", lambda e: e.memset(epsb[:, :], EPS), sig=True)
        t_g32 = t_g

        xt_r, xn_r, hT_r, qk_r, tp_r, acc_r, rt_r = (Rot(xt), Rot(xn), Rot(hT), Rot(qk), Rot(tp),
                                                      Rot(acc), Rot(rt))
        xv = io["x_own"]
        for m in range(NT):
            kx, xb, fr = xt_r.next()
            t_ld = c.dma("sp", xb[:, :], xv[m, :, :], f"ld_x{kx}", waits=fr)
            t_sq = c.op("act", lambda e, xb=xb, m=m: e.activation(out=junk[:, :], in_=xb[:, :], func=AF.Square,
                                                              accum_out=ss[:, m:m + 1]),
                        waits=[t_ld, t_z], sig=True)
            t_e = c.op("act", lambda e, m=m: e.activation(out=sse[:, m:m + 1], in_=ss[:, m:m + 1], func=AF.Ln,
                                                        scale=1.0 / D, bias=epsb[:, :]), waits=[t_sq, t_nh], sig=True)
            t_r = c.op("act", lambda e, m=m: e.activation(out=rstd[:, m:m + 1], in_=sse[:, m:m + 1], func=AF.Exp,
                                                        scale=-0.5), waits=[t_e], sig=True)
            kn, xnb, fr = xn_r.next()
            t_xn = c.op("dve", lambda e, xb=xb, xnb=xnb, m=m: e.scalar_tensor_tensor(
                out=xnb[:, :], in0=xb[:, :], scalar=rstd[:, m:m + 1], in1=grow[:, :],
                op0=ALU.mult, op1=ALU.mult), waits=[t_r, t_g32] + fr, sig=True)
            xt_r.release(kx, t_sq, t_xn)
            if STAGE <= 1:
                continue
            kt, tpb, fr = tp_r.next()
            for cc in range(8):
                t_tp = c.op("pe", lambda e, tpb=tpb, xnb=xnb, cc=cc: e.transpose(
                    out=tpb[:, cc * 128:(cc + 1) * 128], in_=xnb[:, cc * 128:(cc + 1) * 128], identity=ident[:, :]),
                    waits=[t_xn, t_id] + fr, sig=(True if cc == 7 else None))
            xn_r.release(kn, t_tp)
            kh, hb, fr = hT_r.next()
            t_h = c.op("act", lambda e, hb=hb, tpb=tpb: e.activation(
                out=hb[:, :, :], in_=tpb[:, :].rearrange("p (c t) -> p c t", c=8), func=AF.Copy),
                waits=[t_tp] + fr, sig=True)
            tp_r.release(kt, t_h)
            if STAGE <= 2:
                continue
            kq, qb, frq = qk_r.next()
            t_lastmm = None
            for grp in range(3):
                ka, ab, fr = acc_r.next()
                for half in range(2):
                    for cc in range(8):
                        t_mm = c.op("pe", lambda e, ab=ab, hb=hb, cc=cc, half=half, grp=grp: e.matmul(
                            ab[:, half * 512:(half + 1) * 512], lhsT=hb[:, cc, :],
                            rhs=wq[:, cc, grp * 1024 + half * 512: grp * 1024 + (half + 1) * 512],
                            start=(cc == 0), stop=(cc == 7)),
                            waits=[t_h, t_w[cc]] + fr, sig=(True if (half == 1 and cc == 7) else None))
                t_lastmm = t_mm
                if STAGE <= 3:
                    continue
                a3 = ab[:, :].rearrange("p (s d) -> p s d", d=64)
                if grp < 2:
                    q3 = qb[:, grp, :].rearrange("p (s d) -> p s d", d=64)
                    kr, rb, frr = rt_r.next()
                    cb = cos_sb[:, m:m + 1, :].to_broadcast([128, 16, 8])
                    sbb = sin_sb[:, m:m + 1, :].to_broadcast([128, 16, 8])
                    x1 = a3[:, :, 0:8]
                    x2 = a3[:, :, 8:16]
                    w0 = [t_mm, t_cs, t_sn] + frr + frq
                    t1 = c.op("dve", lambda e, o=rb[0], a=x1, b=cb: e.tensor_tensor(out=o[:, :, :], in0=a, in1=b, op=ALU.mult), waits=w0, sig=True)
                    t2 = c.op("dve", lambda e, o=rb[1], a=x2, b=sbb: e.tensor_tensor(out=o[:, :, :], in0=a, in1=b, op=ALU.mult), waits=w0, sig=True)
                    t3 = c.op("dve", lambda e, o=rb[2], a=x2, b=cb: e.tensor_tensor(out=o[:, :, :], in0=a, in1=b, op=ALU.mult), waits=w0, sig=True)
                    t4 = c.op("dve", lambda e, o=rb[3], a=x1, b=sbb: e.tensor_tensor(out=o[:, :, :], in0=a, in1=b, op=ALU.mult), waits=w0, sig=True)
                    t5 = c.op("dve", lambda e, o=q3[:, :, 0:8], a=rb[0], b=rb[1]: e.tensor_tensor(out=o, in0=a[:, :, :], in1=b[:, :, :], op=ALU.subtract), waits=[t1, t2], sig=True)
                    t6 = c.op("dve", lambda e, o=q3[:, :, 8:16], a=rb[2], b=rb[3]: e.tensor_tensor(out=o, in0=a[:, :, :], in1=b[:, :, :], op=ALU.add), waits=[t3, t4], sig=True)
                    rt_r.release(kr, t5, t6)
                    t7 = c.op("act", lambda e, o=q3[:, :, 16:64], a=a3[:, :, 16:64]: e.activation(out=o, in_=a, func=AF.Copy),
                              waits=[t_mm, t4] + frq, sig=True)
                    acc_r.release(ka, t4, t7)
                    if STAGE <= 4:
                        continue
                    kt2, tpb2, fr2 = tp_r.next()
                    for h in range(NH):
                        t_tq = c.op("pe", lambda e, tpb2=tpb2, qb=qb, grp=grp, h=h: e.transpose(
                            out=tpb2[:, h * 128:(h + 1) * 128], in_=qb[:, grp, h * 128:(h + 1) * 128], identity=ident[:, :]),
                            waits=[t5, t6, t7] + fr2, sig=(True if h == 7 else None))
                    dst = (QT if grp == 0 else KT)
                    t_ev = c.op("dve", lambda e, dst=dst, tpb2=tpb2, m=m: e.tensor_copy(
                        out=dst[:, :, m * 128:(m + 1) * 128], in_=tpb2[:, :].rearrange("p (h t) -> p h t", h=8)),
                        waits=[t_tq], sig=True)
                    tp_r.release(kt2, t_ev)
                    if grp == 1:
                        qk_r.release(kq, t_tq)
                else:
                    t_v = c.op("act", lambda e, ab=ab, m=m: e.activation(
                        out=Vs[:, :, m, :], in_=ab[:, :].rearrange("p (h e) -> p h e", h=8), func=AF.Copy),
                        waits=[t_mm], sig=True)
                    acc_r.release(ka, t_v)
            hT_r.release(kh, t_lastmm)
        c.flush()


def phase2(c, nc, io, QT, o_sb, kTg, vg):
    QC = min(4, NT)
    NCH = NT // QC
    st = contextlib.ExitStack()
    with st:
        kt = [sb(nc, st, f"kt{i}", [128, 4, TOK], BF16) for i in range(2)]
        vt = [sb(nc, st, f"vt{i}", [128, 4, NT, 132], BF16) for i in range(2)]
        Pt = [sb(nc, st, f"Pt{i}", [128, 2, 512], BF16) for i in range(3)]
        masks = sb(nc, st, "masks", [128, 4, 128], BF16)
        lamv = sb(nc, st, "lamv", [128, 4, 64], F32)
        lpr = sb(nc, st, "lpr", [128, 2, 64], F32)
        ljunk = sb(nc, st, "ljunk", [128, 64], F32)
        lsum = sb(nc, st, "lsum", [128, 2], F32)
        lexp = sb(nc, st, "lexp", [128, 2], F32)
        neglam = sb(nc, st, "neglam", [128, 1], F32)
        gsub = sb(nc, st, "gsub", [128, 128], F32)
        epsb = sb(nc, st, "epsb2", [128, 1], F32)
        rec = [sb(nc, st, f"rec{i}", [128, 2, 1], F32) for i in range(2)]
        lamrec = [sb(nc, st, f"lamrec{i}", [128, 1], F32) for i in range(2)]
        t0b = [sb(nc, st, f"t0b{i}", [128, 128], F32) for i in range(2)]
        ddb = [sb(nc, st, f"ddb{i}", [128, 128], F32) for i in range(2)]
        sjunk = sb(nc, st, "sjunk", [128, 128], BF16)
        ssq = sb(nc, st, "ssq", [128, NH * NT], F32)
        sln = sb(nc, st, "sln", [128, NH * NT], F32)
        srs = sb(nc, st, "srs", [128, NH * NT], F32)
        Sps = [ps(nc, st, f"Sps{i}", [128, 1024], F32) for i in range(2)]
        Ops = ps(nc, st, "Ops", [128, 4 * 512], F32)

        t_m = c.dma("pool", masks[:, :, :], io["masks"].rearrange("p (r q) -> p r q", r=4), "ld_m")
        for k, nm in enumerate(("lq1", "lk1", "lq2", "lk2")):
            t_l = c.dma("sp", lamv[:, k, :], io[nm][0:1, :].to_broadcast([128, 64]), "ld_lam")
        t_gs = c.dma("sp", gsub[:, :], io["subln_g"][0:1, :].to_broadcast([128, 128]), "ld_gs")
        t_e = c.op("dve", lambda e: e.memset(epsb[:, :], EPS), sig=True)
        t_z = c.op("dve", lambda e: e.memset(ssq[:, :], 0.0), sig=True)
        t_z2 = c.op("dve", lambda e: e.memset(lsum[:, :], 0.0), sig=True)
        for i in range(2):
            t_on = c.op("pool", lambda e, i=i: e.memset(vt[i][:, :, :, 128:129], 1.0), sig=True)
        t_p1 = c.op("dve", lambda e: e.tensor_tensor(out=lpr[:, 0, :], in0=lamv[:, 0, :], in1=lamv[:, 1, :], op=ALU.mult), waits=[t_l], sig=True)
        t_p2 = c.op("dve", lambda e: e.tensor_tensor(out=lpr[:, 1, :], in0=lamv[:, 2, :], in1=lamv[:, 3, :], op=ALU.mult), waits=[t_l], sig=True)
        t_s1 = c.op("act", lambda e: e.activation(out=ljunk[:, :], in_=lpr[:, 0, :], func=AF.Copy, accum_out=lsum[:, 0:1]), waits=[t_p1, t_p2, t_z2], sig=True)
        t_s2 = c.op("act", lambda e: e.activation(out=ljunk[:, :], in_=lpr[:, 1, :], func=AF.Copy, accum_out=lsum[:, 1:2]), waits=[t_s1], sig=True)
        t_ex = c.op("act", lambda e: e.activation(out=lexp[:, :], in_=lsum[:, :], func=AF.Exp), waits=[t_s2], sig=True)
        t_nl = c.op("dve", lambda e: e.scalar_tensor_tensor(out=neglam[:, :], in0=lexp[:, 1:2], scalar=-float(LAMBDA_INIT0),
                                                            in1=lexp[:, 0:1], op0=ALU.add, op1=ALU.subtract), waits=[t_ex], sig=True)
        t_gs2 = c.op("dve", lambda e: e.tensor_scalar(out=gsub[:, :], in0=gsub[:, :], scalar1=float(1.0 - LAMBDA_INIT0),
                                                      scalar2=None, op0=ALU.mult), waits=[t_gs], sig=True)

        kview = kTg.rearrange("(r h p) t -> h p r t", r=4, h=NH)
        vview = vg.rearrange("(r h p) (m e) -> r h p m e", r=4, h=NH, e=128)

        steps = []
        for h in range(NH):
            for ci in range(NCH):
                lst = []
                for mp in range(QC * ci + QC):
                    a = mp - QC * ci if mp >= QC * ci else None
                    for r in range(4):
                        lst.append([h, ci, r, mp, a, False, False])
                lst[0][5] = True
                lst[-1][6] = True
                steps.extend(lst)

        S_r, P_r, rec_r, lr_r, t0_r, dd_r = Rot(Sps), Rot(Pt), Rot(rec), Rot(lamrec), Rot(t0b), Rot(ddb)
        kv_free = [[], []]
        kv_tok = {}
        state = {"memset": None, "evac_d": []}
        pend = None

        def load_kv(h):
            slot = h % 2
            fr = kv_free[slot]
            kv_free[slot] = []
            tk = c.dma("sp", kt[slot][:, :, :], kview[h], f"ld_k{slot}", waits=fr)
            for r in range(4):
                tv = c.dma("sp", vt[slot][:, r, :, 0:128], vview[r, h], f"ld_v{slot}", waits=fr + [t_on])
            kv_tok[h] = (tk, tv)

        def emit_av(stp, kp, pb, t_p):
            h, ci, r, mp, a, first, last = stp
            slot = h % 2
            a0 = a if a is not None else 0
            waits = [t_p, kv_tok[h][1]]
            if first:
                t_ms = c.op("dve", lambda e: e.memset(Ops[:, :], 0.0), waits=state["evac_d"], sig=True)
                state["evac_d"] = []
                waits.append(t_ms)
            t_av = None
            for qi in range(a0, QC):
                for sub in range(2):
                    is_last = (qi == QC - 1 and sub == 1)
                    t_av = c.op("pe", lambda e, qi=qi, sub=sub, pb=pb, a0=a0, slot=slot, r=r, mp=mp: e.matmul(
                        Ops[:, qi * 512 + sub * 256: qi * 512 + sub * 256 + 129],
                        lhsT=pb[:, sub, (qi - a0) * 128:(qi - a0 + 1) * 128],
                        rhs=vt[slot][:, r, mp, 0:129], start=False, stop=False, skip_group_check=True),
                        waits=waits, sig=(True if is_last else None))
            P_r.release(kp, t_av)
            if last and ci == NCH - 1:
                kv_last[h] = t_av
            if last:
                for qi in range(QC):
                    m = ci * QC + qi
                    idx = h * NT + m
                    Ob = Ops[:, qi * 512:(qi + 1) * 512].rearrange("p (s c) -> p s c", s=2)
                    k1, rb, f1 = rec_r.next()
                    t_rc = c.op("dve", lambda e, rb=rb, Ob=Ob: e.reciprocal(out=rb[:, :, :], in_=Ob[:, :, 128:129]), waits=[t_av] + f1, sig=True)
                    k2, lb, f2 = lr_r.next()
                    t_lr = c.op("dve", lambda e, lb=lb, rb=rb: e.tensor_tensor(out=lb[:, :], in0=rb[:, 1, :], in1=neglam[:, :], op=ALU.mult), waits=[t_rc, t_nl] + f2, sig=True)
                    k3, tb, f3 = t0_r.next()
                    t_t0 = c.op("dve", lambda e, tb=tb, rb=rb, Ob=Ob: e.tensor_scalar(out=tb[:, :], in0=Ob[:, 0, 0:128], scalar1=rb[:, 0, :], scalar2=None, op0=ALU.mult), waits=[t_rc] + f3, sig=True)
                    k4, db, f4 = dd_r.next()
                    t_d = c.op("dve", lambda e, db=db, lb=lb, tb=tb, Ob=Ob: e.scalar_tensor_tensor(out=db[:, :], in0=Ob[:, 1, 0:128], scalar=lb[:, :], in1=tb[:, :], op0=ALU.mult, op1=ALU.add), waits=[t_lr, t_t0] + f4, sig=True)
                    rec_r.release(k1, t_d)
                    lr_r.release(k2, t_d)
                    t0_r.release(k3, t_d)
                    state["evac_d"].append(t_d)
                    t_sq = c.op("act", lambda e, db=db, idx=idx: e.activation(out=sjunk[:, :], in_=db[:, :], func=AF.Square, accum_out=ssq[:, idx:idx + 1]), waits=[t_d, t_z], sig=True)
                    t_ln = c.op("act", lambda e, idx=idx: e.activation(out=sln[:, idx:idx + 1], in_=ssq[:, idx:idx + 1], func=AF.Ln, scale=1.0 / 128, bias=epsb[:, :]), waits=[t_sq, t_e], sig=True)
                    t_rs = c.op("act", lambda e, idx=idx: e.activation(out=srs[:, idx:idx + 1], in_=sln[:, idx:idx + 1], func=AF.Exp, scale=-0.5), waits=[t_ln], sig=True)
                    t_o = c.op("dve", lambda e, db=db, idx=idx, m=m, h=h: e.scalar_tensor_tensor(out=o_sb[:, m, h * 128:(h + 1) * 128], in0=db[:, :], scalar=srs[:, idx:idx + 1], in1=gsub[:, :], op0=ALU.mult, op1=ALU.mult), waits=[t_rs, t_gs2], sig=True)
                    dd_r.release(k4, t_o)

        kv_last = {}
        load_kv(0)
        for i, stp in enumerate(steps):
            h, ci, r, mp, a, first, last = stp
            slot = h % 2
            if first and ci == 0 and h + 1 < NH:
                if h >= 1:
                    kv_free[(h + 1) % 2] = [kv_last[h - 1]] if (h - 1) in kv_last else []
                if (h - 1) in kv_last or h == 0:
                    load_kv(h + 1)
                else:
                    state["need_load"] = h + 1
            a0 = a if a is not None else 0
            n = (QC - a0) * 128
            q0 = ci * QC * 128 + a0 * 128
            ks, Sb, fs = S_r.next()
            for sub in range(2):
                t_qk = c.op("pe", lambda e, Sb=Sb, sub=sub, slot=slot, r=r, mp=mp, h=h, q0=q0, n=n: e.matmul(
                    Sb[:, sub * 512: sub * 512 + n],
                    lhsT=kt[slot][sub * 64:(sub + 1) * 64, r, mp * 128:(mp + 1) * 128],
                    rhs=QT[sub * 64:(sub + 1) * 64, h, q0:q0 + n], start=True, stop=True),
                    waits=[kv_tok[h][0]] + fs, sig=(True if sub == 1 else None))
            kp, pb, fp = P_r.next()
            t_p = c.op("act", lambda e, Sb=Sb, pb=pb, n=n: e.activation(
                out=pb[:, :, 0:n], in_=Sb[:, :].rearrange("p (s q) -> p s q", s=2)[:, :, 0:n], func=AF.Exp, scale=0.125),
                waits=[t_qk] + fp, sig=True)
            S_r.release(ks, t_p)
            if a is not None:
                t_p = c.op("dve", lambda e, pb=pb, r=r: e.tensor_tensor(
                    out=pb[:, :, 0:128], in0=pb[:, :, 0:128], in1=masks[:, r:r + 1, :].to_broadcast([128, 2, 128]), op=ALU.mult),
                    waits=[t_p, t_m], sig=True)
            if pend is not None:
                emit_av(*pend)
                if state.get("need_load") is not None and (state["need_load"] - 2) in kv_last:
                    hh = state.pop("need_load")
                    kv_free[hh % 2] = [kv_last[hh - 2]]
                    load_kv(hh)
            pend = (stp, kp, pb, t_p)
        emit_av(*pend)
        c.flush()


NB = 256


class FM:
    def __init__(self, c, nc, st, io):
        self.c, self.nc, self.io, self.st = c, nc, io, st
        self.xT = sb(nc, st, "xT", [128, 8, TOK], F32)
        self.ident_f = sb(nc, st, "ident_f", [128, 128], F32)
        self.ident_b = sb(nc, st, "ident_b", [128, 128], BF16)
        self.ones_b = sb(nc, st, "ones_b", [128, 128], BF16)
        self.epsb = sb(nc, st, "epsb3", [128, 1], F32)
        self.xt = [sb(nc, st, f"fxt{i}", [128, 1024], F32) for i in range(2)]
        self.hT = [sb(nc, st, f"fhT{i}", [128, 8, NB], BF16) for i in range(2)]
        self.aT = sb(nc, st, "faT", [128, 32, NB], BF16)
        self.wblk = [sb(nc, st, f"wblk{i}", [128, 32, 256], BF16) for i in range(2)]
        self.mt = sb(nc, st, "fmt", [128, 8, NB], F32)
        self.sq = sb(nc, st, "fsq", [128, 8, NB], BF16)
        self.ln = sb(nc, st, "fln", [128, NB], F32)
        self.rb = sb(nc, st, "frb", [128, NB], F32)
        self.rl = [sb(nc, st, f"frl{i}", [128, NB], BF16) for i in range(2)]
        self.pm = [ps(nc, st, f"pm{i}", [128, 512], F32) for i in range(4)]
        self.pn = ps(nc, st, "pn", [128, 512], F32)
        self.pt = ps(nc, st, "ptr", [128, 1024], F32)
        self.ptb = ps(nc, st, "ptrb", [128, 1024], BF16)
        self.xt_r, self.hT_r, self.w_r, self.pm_r, self.rl_r = Rot(self.xt), Rot(self.hT), Rot(self.wblk), Rot(self.pm), Rot(self.rl)
        self.pt_free, self.ptb_free, self.pn_free = [], [], []
        self.mt_free, self.sq_free, self.rb_free, self.aT_free = [], [], [], []
        self.t_idf = c.dma("sp", self.ident_f[:, :], io["ident"][:, :], "ld_idf")
        self.t_idb = c.dma("pool", self.ident_b[:, :], io["ident"][:, :], "ld_idb")
        self.t_one = c.op("dve", lambda e: e.memset(self.ones_b[:, :], 1.0), sig=True)
        self.t_eps = c.op("dve", lambda e: e.memset(self.epsb[:, :], EPS), sig=True)
        self.gains = {}

    def gain(self, name):
        t = sb(self.nc, self.st, "g_" + name, [128, 8], F32)
        tok = self.c.dma("sp", t[:, :], self.io[name][:, :], "ld_g_" + name)
        self.gains[name] = (t, tok)

    def load_x(self, src):
        c = self.c
        for m in range(NT):
            kx, xb, fr = self.xt_r.next()
            t_ld = c.dma("sp", xb[:, :], src[m, :, :], f"ld_fx{kx}", waits=fr)
            for cc in range(8):
                t_tp = c.op("pe", lambda e, xb=xb, cc=cc: e.transpose(out=self.pt[:, cc * 128:(cc + 1) * 128], in_=xb[:, cc * 128:(cc + 1) * 128], identity=self.ident_f[:, :]),
                            waits=[t_ld, self.t_idf] + self.pt_free, sig=(True if cc == 7 else None))
            self.xt_r.release(kx, t_tp)
            t_ev = c.op("dve", lambda e, m=m: e.tensor_copy(out=self.xT[:, :, m * 128:(m + 1) * 128], in_=self.pt[:, :].rearrange("p (c t) -> p c t", c=8)), waits=[t_tp], sig=True)
            self.pt_free = [t_ev]
        return t_ev

    def store_x(self, dst, waits):
        c = self.c
        t_st = None
        for m in range(NT):
            kx, xb, fr = self.xt_r.next()
            for cc in range(8):
                t_tp = c.op("pe", lambda e, m=m, cc=cc: e.transpose(out=self.pt[:, cc * 128:(cc + 1) * 128], in_=self.xT[:, cc, m * 128:(m + 1) * 128], identity=self.ident_f[:, :]),
                            waits=list(waits) + [self.t_idf] + self.pt_free, sig=(True if cc == 7 else None))
            t_ev = c.op("dve", lambda e, xb=xb: e.tensor_copy(out=xb[:, :], in_=self.pt[:, :]), waits=[t_tp] + fr, sig=True)
            self.pt_free = [t_ev]
            t_st = c.dma("sp", dst[m, :, :], xb[:, :], f"st_fx{kx}", waits=[t_ev])
            self.xt_r.release(kx, t_st)
        return [("st_fx0", c.cnt["st_fx0"]), ("st_fx1", c.cnt["st_fx1"])]

    def norm(self, src, gname, waits):
        c = self.c
        g, t_g = self.gains[gname]
        t_sq = c.op("act", lambda e: e.activation(out=self.sq[:, :, :], in_=src, func=AF.Square), waits=list(waits) + self.sq_free, sig=True)
        for cc in range(8):
            t_mm = c.op("pe", lambda e, cc=cc: e.matmul(self.pn[:, 0:NB], lhsT=self.ones_b[:, :], rhs=self.sq[:, cc, :], start=(cc == 0), stop=(cc == 7)),
                        waits=[t_sq, self.t_one] + self.pn_free, sig=(True if cc == 7 else None))
        self.sq_free = [t_mm]
        t_ln = c.op("act", lambda e: e.activation(out=self.ln[:, :], in_=self.pn[:, 0:NB], func=AF.Ln, scale=1.0 / D, bias=self.epsb[:, :]), waits=[t_mm, self.t_eps] + self.rb_free, sig=True)
        self.pn_free = [t_ln]
        t_rb = c.op("act", lambda e: e.activation(out=self.rb[:, :], in_=self.ln[:, :], func=AF.Exp, scale=-0.5), waits=[t_ln], sig=True)
        kh, hb, fr = self.hT_r.next()
        for cc in range(8):
            t_h = c.op("dve", lambda e, cc=cc, hb=hb: e.scalar_tensor_tensor(out=hb[:, cc, :], in0=src[:, cc, :], scalar=g[:, cc:cc + 1], in1=self.rb[:, :], op0=ALU.mult, op1=ALU.mult),
                       waits=[t_rb, t_g] + fr, sig=True)
        self.rb_free = [t_h]
        return kh, hb, t_h

    def linear(self, W, KC, n_mt, inT, t_in, evac):
        c = self.c
        Wv = W.rearrange("(kc p) n -> p kc n", p=128)
        t_last = None
        for g in range(n_mt // 2):
            kw, wb, fr = self.w_r.next()
            t_w = c.dma("pool", wb[:, 0:KC, :], Wv[:, :, g * 256:(g + 1) * 256], f"ld_wb{kw}", waits=fr)
            for j in range(2):
                mt = g * 2 + j
                kp, pb, fp = self.pm_r.next()
                for kc in range(KC):
                    t_mm = c.op("pe", lambda e, pb=pb, wb=wb, kc=kc, j=j: e.matmul(pb[:, 0:NB], lhsT=wb[:, kc, j * 128:(j + 1) * 128], rhs=inT[:, kc, :], start=(kc == 0), stop=(kc == KC - 1)),
                                waits=[t_w, t_in] + fp, sig=(True if kc == KC - 1 else None))
                t_free = evac(mt, pb[:, 0:NB], t_mm)
                self.pm_r.release(kp, t_free)
                t_last = t_mm
            self.w_r.release(kw, t_last)
        return t_last

    def evac_resid(self):
        c = self.c
        toks = []
        fr0 = list(self.mt_free)

        def ev(mt, pap, t_mm):
            t_c = c.op("act", lambda e, mt=mt, pap=pap: e.activation(out=self.mt[:, mt, :], in_=pap, func=AF.Copy), waits=[t_mm] + fr0, sig=True)
            toks.append(t_c)
            return t_c
        return ev, toks

    def postnorm_add(self, gname, cols, toks):
        c = self.c
        g, t_g = self.gains[gname]
        t_sq = c.op("act", lambda e: e.activation(out=self.sq[:, :, :], in_=self.mt[:, :, :], func=AF.Square), waits=list(toks) + self.sq_free, sig=True)
        for cc in range(8):
            t_mm = c.op("pe", lambda e, cc=cc: e.matmul(self.pn[:, 0:NB], lhsT=self.ones_b[:, :], rhs=self.sq[:, cc, :], start=(cc == 0), stop=(cc == 7)),
                        waits=[t_sq, self.t_one] + self.pn_free, sig=(True if cc == 7 else None))
        self.sq_free = [t_mm]
        t_ln = c.op("act", lambda e: e.activation(out=self.ln[:, :], in_=self.pn[:, 0:NB], func=AF.Ln, scale=1.0 / D, bias=self.epsb[:, :]), waits=[t_mm, self.t_eps] + self.rb_free, sig=True)
        self.pn_free = [t_ln]
        t_rb = c.op("act", lambda e: e.activation(out=self.rb[:, :], in_=self.ln[:, :], func=AF.Exp, scale=-0.5), waits=[t_ln], sig=True)
        t_a = None
        for cc in range(8):
            t_s = c.op("dve", lambda e, cc=cc: e.scalar_tensor_tensor(out=self.mt[:, cc, :], in0=self.mt[:, cc, :], scalar=g[:, cc:cc + 1], in1=self.rb[:, :], op0=ALU.mult, op1=ALU.mult),
                       waits=[t_rb, t_g], sig=True)
            t_a = c.op("pool", lambda e, cc=cc: e.tensor_tensor(out=self.xT[:, cc, cols], in0=self.xT[:, cc, cols], in1=self.mt[:, cc, :], op=ALU.add), waits=[t_s], sig=True)
        self.rb_free = [t_s]
        self.mt_free = [t_a]
        return t_a

    def mlp(self, cols, layer, g_pre, g_post, t_x):
        c = self.c
        kh, hb, t_h = self.norm(self.xT[:, :, cols], g_pre, [t_x])
        fr_a = list(self.aT_free)

        def ev_up(mt, pap, t_mm):
            kr, rbuf, fr = self.rl_r.next()
            t_r = c.op("act", lambda e, rbuf=rbuf, pap=pap: e.activation(out=rbuf[:, :], in_=pap, func=AF.Relu), waits=[t_mm] + fr, sig=True)
            t_q = c.op("dve", lambda e, rbuf=rbuf, mt=mt: e.tensor_tensor(out=self.aT[:, mt, :], in0=rbuf[:, :], in1=rbuf[:, :], op=ALU.mult), waits=[t_r] + fr_a, sig=True)
            self.rl_r.release(kr, t_q)
            ev_up.last = t_q
            return t_r
        t_up = self.linear(self.io["w_up"], 8, 32, hb, t_h, ev_up)
        self.hT_r.release(kh, t_up)
        ev, toks = self.evac_resid()
        t_dn = self.linear(self.io["w_down"], 32, 8, self.aT, ev_up.last, ev)
        self.aT_free = [t_dn]
        return self.postnorm_add(g_post, cols, toks)


def phase3(c, nc, io, o_sb, out_ap):
    st = contextlib.ExitStack()
    with st:
        fm = FM(c, nc, st, io)
        for nm in ("g_post0", "g_mlp_pre", "g_mlp_post"):
            fm.gain(nm)
        t_x = fm.load_x(io["x_own"])
        t_fin = None
        for blk in range(TOK // NB):
            cols = slice(blk * NB, (blk + 1) * NB)
            kh, hb, fr = fm.hT_r.next()
            t_e = None
            for ti in range(NB // 128):
                m = blk * (NB // 128) + ti
                for cc in range(8):
                    t_tp = c.op("pe", lambda e, m=m, cc=cc: e.transpose(out=fm.ptb[:, cc * 128:(cc + 1) * 128], in_=o_sb[:, m, cc * 128:(cc + 1) * 128], identity=fm.ident_b[:, :]),
                                waits=[fm.t_idb] + fm.ptb_free, sig=(True if cc == 7 else None))
                t_e = c.op("dve", lambda e, hb=hb, ti=ti: e.tensor_copy(out=hb[:, :, ti * 128:(ti + 1) * 128], in_=fm.ptb[:, :].rearrange("p (c t) -> p c t", c=8)), waits=[t_tp] + fr, sig=True)
                fm.ptb_free = [t_e]
            ev, toks = fm.evac_resid()
            t_l = fm.linear(io["w_o"], 8, 8, hb, t_e, ev)
            fm.hT_r.release(kh, t_l)
            t_a = fm.postnorm_add("g_post0", cols, toks + [t_x])
            t_fin = fm.mlp(cols, 0, "g_mlp_pre", "g_mlp_post", t_a)
        toks = fm.store_x(out_ap, [t_fin])
        c.wait("sp", toks)
        c.flush()


def phase4(c, nc, io, out_ap):
    st = contextlib.ExitStack()
    with st:
        fm = FM(c, nc, st, io)
        for nm in ("g_pre1", "g_post1", "g_mlp_pre", "g_mlp_post"):
            fm.gain(nm)
        NBT = NB // 128
        cw = sb(nc, st, "cw", [128, 8, 3], F32)
        t_cw = c.dma("sp", cw[:, :, :], io["conv_w"].rearrange("p (c k) -> p c k", k=3), "ld_cw")
        xh = sb(nc, st, "xh", [128, 8, NB], F32)
        uh = sb(nc, st, "uh", [128, 8, NT, 2], F32)
        ue = sb(nc, st, "ue", [128, 8, NBT, 130], F32)
        cT = sb(nc, st, "cT", [128, 8, NB], F32)
        bT = sb(nc, st, "bT", [128, 8, NB], BF16)
        yT = sb(nc, st, "yT", [128, NBT, 128], F32)
        zT = sb(nc, st, "zT", [128, 8, NB], BF16)
        t_x = fm.load_x(io["x_own"])
        t_z = c.op("dve", lambda e: e.memset(xh[:, :, :], 0.0), sig=True)
        t_hl = c.dma("sp", xh[:, :, 0:2 * NT], io["x_halo"].rearrange("p (c t) -> p c t", c=8), "ld_xh", waits=[t_z])
        kh, hb, t_h = fm.norm(xh[:, :, :], "g_pre1", [t_hl])
        state = {}

        def ev_in(uview, cols_n):
            def ev(mt, pap, t_mm):
                if mt < 8:
                    t = c.op("act", lambda e, mt=mt, pap=pap: e.activation(out=cT[:, mt, 0:cols_n], in_=pap[:, 0:cols_n], func=AF.Copy), waits=[t_mm] + state.get("cT_free", []), sig=True)
                    state[("c", mt)] = t
                    return t
                cc = mt - 8
                t = c.op("dve", lambda e, cc=cc, pap=pap: e.tensor_tensor(out=uview(cc), in0=pap[:, 0:cols_n] if uview(cc).ndim == 2 else pap[:, 0:cols_n].rearrange("p (t k) -> p t k", k=uview(cc).shape[-1]), in1=cT[:, cc, 0:cols_n] if uview(cc).ndim == 2 else cT[:, cc, 0:cols_n].rearrange("p (t k) -> p t k", k=uview(cc).shape[-1]), op=ALU.mult),
                         waits=[t_mm, state[("c", cc)]] + state.get("u_free", []), sig=True)
                state["u_last"] = t
                return t
            return ev
        W_cu = io["w_in"][:, 1024:3072]
        t_l = fm.linear(W_cu, 8, 16, hb, t_h, ev_in(lambda cc: uh[:, cc, :, :], 2 * NT))
        fm.hT_r.release(kh, t_l)
        t_uh = state["u_last"]
        t_fin = None
        for blk in range(TOK // NB):
            cols = slice(blk * NB, (blk + 1) * NB)
            kh, hb, t_h = fm.norm(fm.xT[:, :, cols], "g_pre1", [t_x] + ([t_fin] if t_fin else []))
            state["cT_free"] = [state["u_last"]]
            state["u_free"] = state.get("conv_done", [])
            t_l = fm.linear(W_cu, 8, 16, hb, t_h, ev_in(lambda cc: ue[:, cc, :, 2:130], NB))
            t_u = state["u_last"]
            def ev_b(mt, pap, t_mm):
                t = c.op("act", lambda e, mt=mt, pap=pap: e.activation(out=bT[:, mt, :], in_=pap, func=AF.Copy), waits=[t_mm] + state.get("b_free", []), sig=True)
                state["b_last"] = t
                return t
            t_l2 = fm.linear(io["w_in"][:, 0:1024], 8, 8, hb, t_h, ev_b)
            fm.hT_r.release(kh, t_l2)
            t_hc = c.op("dve", lambda e, blk=blk: e.tensor_copy(out=ue[:, :, :, 0:2], in_=uh[:, :, blk * NBT:(blk + 1) * NBT, :]), waits=[t_uh] + state.get("conv_done", []), sig=True)
            t_zz = None
            for cc in range(8):
                t1 = c.op("dve", lambda e, cc=cc: e.tensor_scalar(out=yT[:, :, :], in0=ue[:, cc, :, 0:128], scalar1=cw[:, cc, 0:1], scalar2=None, op0=ALU.mult), waits=[t_u, t_hc, t_cw] + ([t_zz] if t_zz else []), sig=True)
                t2 = c.op("dve", lambda e, cc=cc: e.scalar_tensor_tensor(out=yT[:, :, :], in0=ue[:, cc, :, 1:129], scalar=cw[:, cc, 1:2], in1=yT[:, :, :], op0=ALU.mult, op1=ALU.add), waits=[t1], sig=True)
                t3 = c.op("dve", lambda e, cc=cc: e.scalar_tensor_tensor(out=yT[:, :, :], in0=ue[:, cc, :, 2:130], scalar=cw[:, cc, 2:3], in1=yT[:, :, :], op0=ALU.mult, op1=ALU.add), waits=[t2], sig=True)
                t_zz = c.op("dve", lambda e, cc=cc: e.tensor_tensor(out=zT[:, cc, :].rearrange("p (t k) -> p t k", k=128), in0=yT[:, :, :], in1=bT[:, cc, :].rearrange("p (t k) -> p t k", k=128), op=ALU.mult), waits=[t3, state["b_last"]] + state.get("z_free", []), sig=True)
            state["conv_done"] = [t_zz]
            state["b_free"] = [t_zz]
            ev, toks = fm.evac_resid()
            t_lo = fm.linear(io["w_out"], 8, 8, zT, t_zz, ev)
            state["z_free"] = [t_lo]
            t_a = fm.postnorm_add("g_post1", cols, toks)
            t_fin = fm.mlp(cols, 1, "g_mlp_pre", "g_mlp_post", t_a)
        toks = fm.store_x(out_ap, [t_fin])
        c.wait("sp", toks)
        c.flush()


def _io(nc):
    io = {}

    def din(name, shape, dt=F32):
        io[name] = nc.dram_tensor(name, list(shape), dt, kind="ExternalInput").ap()

    def dout(name, shape, dt=F32):
        io[name] = nc.dram_tensor(name, list(shape), dt, kind="ExternalOutput").ap()
    return io, din, dout


def build_A():
    nc = bass.Bass("TRN2", target_bir_lowering=False)
    io, din, dout = _io(nc)
    din("x_own", [NT, 128, D])
    din("wqkv", [D, 3 * D])
    din("g_pre0_row", [1, D])
    din("cos_t", [128, NT * 8])
    din("sin_t", [128, NT * 8])
    din("ident", [128, 128])
    dout("qT", [NH * 128, TOK], BF16)
    dout("kT", [NH * 128, TOK], BF16)
    dout("vv", [NH * 128, TOK], BF16)
    with contextlib.ExitStack() as stack:
        c = Ctx(nc, stack)
        QT = sb(nc, stack, "QT", [128, NH, TOK], BF16)
        KT = sb(nc, stack, "KT", [128, NH, TOK], BF16)
        Vs = sb(nc, stack, "Vs", [128, NH, NT, 128], BF16)
        phase1(c, nc, io, QT, KT, Vs)
        c.dma("sp", io["qT"].rearrange("(h p) t -> p h t", p=128), QT[:, :, :], "st_o")
        c.dma("sp", io["kT"].rearrange("(h p) t -> p h t", p=128), KT[:, :, :], "st_o")
        t3 = c.dma("sp", io["vv"].rearrange("(h p) (m e) -> p h m e", p=128, e=128), Vs[:, :, :, :], "st_o")
        c.wait("sp", [t3])
        c.flush()
    return nc


def build_B():
    nc = bass.Bass("TRN2", target_bir_lowering=False)
    io, din, dout = _io(nc)
    din("x_own", [NT, 128, D])
    din("qT", [NH * 128, TOK], BF16)
    din("kTg", [4 * NH * 128, TOK], BF16)
    din("vg", [4 * NH * 128, TOK], BF16)
    din("masks", [128, 4 * 128])
    for nm in ("lq1", "lk1", "lq2", "lk2"):
        din(nm, [1, 64])
    din("subln_g", [1, 128])
    din("ident", [128, 128])
    din("w_o", [D, D])
    din("w_up", [D, DFF])
    din("w_down", [DFF, D])
    for nm in ("g_post0", "g_mlp_pre", "g_mlp_post"):
        din(nm, [128, 8])
    dout("x1", [NT, 128, D])
    with contextlib.ExitStack() as stack:
        c = Ctx(nc, stack)
        o_sb = sb(nc, stack, "o_sb", [128, NT, D], BF16)
        with contextlib.ExitStack() as s2:
            QT = sb(nc, s2, "QT", [128, NH, TOK], BF16)
            t_q = c.dma("sp", QT[:, :, :], io["qT"].rearrange("(h p) t -> p h t", p=128), "ld_q")
            c.wait("pe", [t_q])
            phase2(c, nc, io, QT, o_sb, io["kTg"], io["vg"])
        phase3(c, nc, io, o_sb, io["x1"])
    return nc


def build_C():
    nc = bass.Bass("TRN2", target_bir_lowering=False)
    io, din, dout = _io(nc)
    din("x_own", [NT, 128, D])
    din("x_halo", [128, 8 * 2 * NT])
    din("ident", [128, 128])
    din("w_in", [D, 3 * D])
    din("conv_w", [128, 8 * 3])
    din("w_out", [D, D])
    din("w_up", [D, DFF])
    din("w_down", [DFF, D])
    for nm in ("g_pre1", "g_post1", "g_mlp_pre", "g_mlp_post"):
        din(nm, [128, 8])
    dout("x2", [NT, 128, D])
    with contextlib.ExitStack() as stack:
        c = Ctx(nc, stack)
        phase4(c, nc, io, io["x2"])
    return nc


def rope_tables(j):
    inv_freq = (np.float32(500000.0) ** (-(np.arange(0, 16, 2, dtype=np.float32)) / np.float32(16))).astype(np.float32)
    m = np.arange(NT)
    p = np.arange(128)
    pos = (128 * (4 * m[None, :] + j) + p[:, None]).astype(np.float32)
    ang = (pos[:, :, None] * inv_freq[None, None, :]).astype(np.float32)
    return (np.cos(ang).astype(np.float32).reshape(128, NT * 8),
            np.sin(ang).astype(np.float32).reshape(128, NT * 8))


def own_tiles(x_b, j):
    return np.ascontiguousarray(x_b.reshape(-1, 128, x_b.shape[-1])[j::4][:NT])


def causal_masks(j):
    p = np.arange(128)
    tri = (p[:, None] <= p[None, :]).astype(np.float32)
    mk = np.zeros((128, 4, 128), np.float32)
    for r in range(4):
        if r < j:
            mk[:, r, :] = 1.0
        elif r == j:
            mk[:, r, :] = tri
    return mk.reshape(128, 512)


def fm_gain(g):
    return np.ascontiguousarray(g.reshape(8, 128).T)


def run_A(x, attn_w_qkv, norm_mixer_pre):
    nc = build_A()
    in_maps = []
    ident = np.eye(128, dtype=np.float32)
    for cid in range(8):
        b, j = divmod(cid, 4)
        cs, sn = rope_tables(j)
        in_maps.append({
            "x_own": own_tiles(x[b], j),
            "wqkv": np.ascontiguousarray(attn_w_qkv[0]),
            "g_pre0_row": np.ascontiguousarray(norm_mixer_pre[0:1]),
            "cos_t": cs, "sin_t": sn, "ident": ident,
        })
    res = run_bass_kernel_spmd(nc, in_maps, core_ids=list(range(8)))
    return res.results


def kernel(x, attn_w_qkv, attn_w_o, attn_lambda_q1, attn_lambda_k1, attn_lambda_q2, attn_lambda_k2,
           attn_subln_g, conv_w_in, conv_w, conv_w_out, mlp_w_up, mlp_w_down, norm_mixer_pre,
           norm_mixer_post, norm_mlp_pre, norm_mlp_post):
    f = lambda a: np.ascontiguousarray(np.asarray(a, dtype=np.float32))
    x = f(x)
    ident = np.eye(128, dtype=np.float32)
    ra = run_A(x, f(attn_w_qkv), f(norm_mixer_pre))
    ncB = build_B()
    in_maps = []
    for cid in range(8):
        b, j = divmod(cid, 4)
        kTg = np.concatenate([np.asarray(ra[4 * b + r]["kT"]) for r in range(4)], axis=0)
        vg = np.concatenate([np.asarray(ra[4 * b + r]["vv"]) for r in range(4)], axis=0)
        in_maps.append({
            "x_own": own_tiles(x[b], j), "qT": np.asarray(ra[cid]["qT"]), "kTg": kTg, "vg": vg,
            "masks": causal_masks(j),
            "lq1": f(attn_lambda_q1[0:1]), "lk1": f(attn_lambda_k1[0:1]), "lq2": f(attn_lambda_q2[0:1]), "lk2": f(attn_lambda_k2[0:1]),
            "subln_g": f(attn_subln_g[0:1]), "ident": ident,
            "w_o": f(attn_w_o[0]), "w_up": f(mlp_w_up[0]), "w_down": f(mlp_w_down[0]),
            "g_post0": fm_gain(f(norm_mixer_post[0])), "g_mlp_pre": fm_gain(f(norm_mlp_pre[0])),
            "g_mlp_post": fm_gain(f(norm_mlp_post[0])),
        })
    rb = run_bass_kernel_spmd(ncB, in_maps, core_ids=list(range(8))).results
    x1 = np.zeros_like(x)
    for cid in range(8):
        b, j = divmod(cid, 4)
        x1[b].reshape(64, 128, D)[j::4] = np.asarray(rb[cid]["x1"])
    ncC = build_C()
    in_maps = []
    for cid in range(8):
        b, j = divmod(cid, 4)
        halo = np.zeros((NT, 2, D), np.float32)
        for m in range(NT):
            g0 = 128 * (4 * m + j)
            if g0 >= 2:
                halo[m] = x1[b, g0 - 2:g0]
        xh = np.ascontiguousarray(halo.reshape(NT * 2, 8, 128).transpose(2, 1, 0)).reshape(128, 8 * 2 * NT)
        in_maps.append({
            "x_own": own_tiles(x1[b], j), "x_halo": xh, "ident": ident,
            "w_in": f(conv_w_in[0]), "conv_w": np.ascontiguousarray(f(conv_w[0]).reshape(3, 8, 128).transpose(2, 1, 0)).reshape(128, 24),
            "w_out": f(conv_w_out[0]), "w_up": f(mlp_w_up[1]), "w_down": f(mlp_w_down[1]),
            "g_pre1": fm_gain(f(norm_mixer_pre[1])), "g_post1": fm_gain(f(norm_mixer_post[1])),
            "g_mlp_pre": fm_gain(f(norm_mlp_pre[1])), "g_mlp_post": fm_gain(f(norm_mlp_post[1])),
        })
    rc = run_bass_kernel_spmd(ncC, in_maps, core_ids=list(range(8))).results
    out = np.zeros_like(x)
    for cid in range(8):
        b, j = divmod(cid, 4)
        out[b].reshape(64, 128, D)[j::4] = np.asarray(rc[cid]["x2"])
    return out
```
